# Optimizing a Trainium2 kernel written in Bass

```python
import math
import jax
import jax.numpy as jnp
from jax import lax
import numpy as np

D_MODEL = 1024
BATCH = 1
SEQ = 16384
DEPTH = 2

N_META = 16
A_HEADS = 8
A_DH = 64
A_DV = 2 * A_DH
Q_BLOCK = 128
DN_HEADS = 8
DN_DK = 128
DN_DV = 128
DN_CONV = 4
CHUNK = 64
D_FF = 2816
FFN_CONV = 3
RMS_EPS = 1e-6
ALIBI_MAX_EXP = 8.0

A_Q = A_HEADS * 2 * A_DH
A_K = A_HEADS * 2 * A_DH
A_V = A_HEADS * A_DV
DN_QKV = DN_HEADS * (2 * DN_DK + DN_DV)
DN_Z = DN_HEADS * DN_DV
IN_WIDTHS = (A_Q, A_K, A_V, DN_QKV, DN_Z, DN_HEADS, DN_HEADS, D_MODEL, D_MODEL)
D_IN = A_Q + A_K + A_V + DN_QKV + DN_Z + 2 * DN_HEADS + 2 * D_MODEL

kernel_name = 'hybrid_diffattn_gdn_convffn'


def rms_norm(x, g, eps=RMS_EPS):
    xf = x.astype(jnp.float32)
    y = xf * lax.rsqrt(jnp.mean(xf * xf, axis=-1, keepdims=True) + eps)
    return (y * g.astype(jnp.float32)).astype(x.dtype)


def l2_normalize(x, eps=1e-6):
    xf = x.astype(jnp.float32)
    return xf * lax.rsqrt(jnp.sum(xf * xf, axis=-1, keepdims=True) + eps)


def causal_dwconv(x, w):
    k_width = w.shape[0]
    return lax.conv_general_dilated(
        x, w[:, None, :].astype(x.dtype), window_strides=(1,), padding=((k_width - 1, 0),),
        dimension_numbers=('NWC', 'WIO', 'NWC'), feature_group_count=x.shape[-1])


def diff_attention(q, k, v, lam, slopes):
    B, L = q.shape[0], q.shape[1]
    n_blk = -(-L // Q_BLOCK)
    lq = n_blk * Q_BLOCK
    qp = jnp.pad(q, ((0, 0), (0, lq - L), (0, 0), (0, 0), (0, 0)))
    qb = jnp.moveaxis(qp.reshape((B, n_blk, Q_BLOCK) + q.shape[2:]), 1, 0)
    kpos = jnp.arange(L, dtype=jnp.int32)
    scale = A_DH ** -0.5

    def one_block(args):
        q_blk, start = args
        qpos = start + jnp.arange(Q_BLOCK, dtype=jnp.int32)
        s = jnp.einsum('bqhcd,bkhcd->bhcqk', q_blk, k).astype(jnp.float32) * scale
        dist = (qpos[:, None] - kpos[None, :]).astype(jnp.float32)
        bias = -slopes[:, None, None, None] * dist
        s = jnp.where(dist >= 0, s + bias, -jnp.inf)
        p = jax.nn.softmax(s, axis=-1)
        p = p[:, :, 0] - lam * p[:, :, 1]
        return jnp.einsum('bhqk,bkhe->bqhe', p.astype(v.dtype), v)

    starts = jnp.arange(n_blk, dtype=jnp.int32) * Q_BLOCK
    o = lax.map(one_block, (qb, starts))
    o = jnp.moveaxis(o, 0, 1).reshape(B, lq, q.shape[2], v.shape[-1])
    return o[:, :L]


def gated_delta_rule(q, k, v, beta, g):
    B, L, H, DK = q.shape
    front = (CHUNK - N_META % CHUNK) % CHUNK
    back = (-(L + front)) % CHUNK
    lp = L + front + back
    n = lp // CHUNK

    def prep(t):
        pad = ((0, 0), (front, back)) + ((0, 0),) * (t.ndim - 2)
        t = jnp.pad(t, pad).reshape((B, n, CHUNK) + t.shape[2:])
        return jnp.moveaxis(t, 3, 1)

    qc = prep(q * DK ** -0.5)
    kc = prep(k)
    vc = prep(v)
    bc = prep(beta)
    gc = prep(g)
    idx = jnp.arange(CHUNK)
    lower_incl = idx[:, None] >= idx[None, :]
    strict = idx[:, None] > idx[None, :]
    G = jnp.cumsum(gc, axis=-1)
    decay = jnp.exp(jnp.where(lower_incl, G[..., :, None] - G[..., None, :], -jnp.inf))
    kb = kc * bc[..., None]
    vb = vc * bc[..., None]
    A = jnp.where(strict, jnp.einsum('bhnid,bhnjd->bhnij', kb, kc) * decay, 0.0)
    T = A + jnp.eye(CHUNK, dtype=A.dtype)
    u = lax.linalg.triangular_solve(T, vb, left_side=True, lower=True, unit_diagonal=True)
    w = lax.linalg.triangular_solve(T, kb * jnp.exp(G)[..., None], left_side=True, lower=True,
                                    unit_diagonal=True)
    qk = jnp.where(lower_incl, jnp.einsum('bhnid,bhnjd->bhnij', qc, kc) * decay, 0.0)
    q_dec = qc * jnp.exp(G)[..., None]
    k_dec = kc * jnp.exp(G[..., -1:] - G)[..., None]
    chunk_dec = jnp.exp(G[..., -1])

    def step(S, xs):
        u_n, w_n, qd_n, qk_n, kd_n, cd_n = xs
        v_new = u_n - jnp.einsum('bhcd,bhde->bhce', w_n, S)
        o = jnp.einsum('bhcd,bhde->bhce', qd_n, S) + jnp.einsum('bhij,bhje->bhie', qk_n, v_new)
        S = S * cd_n[..., None, None] + jnp.einsum('bhcd,bhce->bhde', kd_n, v_new)
        return S, o

    xs = (jnp.moveaxis(u, 2, 0), jnp.moveaxis(w, 2, 0), jnp.moveaxis(q_dec, 2, 0),
          jnp.moveaxis(qk, 2, 0), jnp.moveaxis(k_dec, 2, 0), jnp.moveaxis(chunk_dec, 2, 0))
    S0 = jnp.zeros((B, H, DK, v.shape[-1]), jnp.float32)
    _, o = lax.scan(step, S0, xs)
    o = o.transpose(1, 0, 3, 2, 4).reshape(B, lp, H, v.shape[-1])
    return o[:, front:front + L]


def setup_inputs(seed: int = 0) -> dict:
    key = jax.random.key(seed)
    ks = jax.random.split(key, 24)
    f32 = jnp.float32

    def normal(k, shape, scale):
        return jax.random.normal(k, shape, f32) * scale

    def gain(k, shape):
        return 1.0 + 0.01 * jax.random.normal(k, shape, f32)

    dt = jnp.exp(jax.random.uniform(ks[13], (DEPTH, DN_HEADS), f32, math.log(1e-3), math.log(1e-1)))
    return {
        'x': normal(ks[0], (BATCH, SEQ, D_MODEL), 1.0),
        'meta_tokens': normal(ks[1], (N_META, D_MODEL), 1.0),
        'mix_norm_g': gain(ks[2], (DEPTH, D_MODEL)),
        'w_in': normal(ks[3], (DEPTH, D_MODEL, D_IN), D_MODEL ** -0.5),
        'q_norm_g': gain(ks[4], (DEPTH, A_DH)),
        'k_norm_g': gain(ks[5], (DEPTH, A_DH)),
        'lambda_q1': normal(ks[6], (DEPTH, A_DH), 0.1),
        'lambda_k1': normal(ks[7], (DEPTH, A_DH), 0.1),
        'lambda_q2': normal(ks[8], (DEPTH, A_DH), 0.1),
        'lambda_k2': normal(ks[9], (DEPTH, A_DH), 0.1),
        'attn_subln_g': gain(ks[10], (DEPTH, A_DV)),
        'dn_conv_w': normal(ks[11], (DEPTH, DN_CONV, DN_QKV), DN_CONV ** -0.5),
        'dn_a_log': jnp.log(jax.random.uniform(ks[12], (DEPTH, DN_HEADS), f32, 1.0, 16.0)),
        'dn_dt_bias': dt + jnp.log(-jnp.expm1(-dt)),
        'dn_norm_g': gain(ks[14], (DEPTH, DN_DV)),
        'w_branch_attn': normal(ks[15], (DEPTH, A_HEADS * A_DV, D_MODEL), (A_HEADS * A_DV) ** -0.5),
        'w_branch_dn': normal(ks[16], (DEPTH, DN_HEADS * DN_DV, D_MODEL), (DN_HEADS * DN_DV) ** -0.5),
        'w_out': normal(ks[17], (DEPTH, D_MODEL, D_MODEL), D_MODEL ** -0.5),
        'ffn_norm_g': gain(ks[18], (DEPTH, D_MODEL)),
        'w_ffn_up': normal(ks[19], (DEPTH, D_MODEL, 2 * D_FF), D_MODEL ** -0.5),
        'ffn_conv_w': normal(ks[20], (DEPTH, FFN_CONV, 2 * D_FF), FFN_CONV ** -0.5),
        'w_ffn_down': normal(ks[21], (DEPTH, D_FF, D_MODEL), D_FF ** -0.5),
    }


def reference(x, meta_tokens, mix_norm_g, w_in, q_norm_g, k_norm_g, lambda_q1, lambda_k1,
              lambda_q2, lambda_k2, attn_subln_g, dn_conv_w, dn_a_log, dn_dt_bias, dn_norm_g,
              w_branch_attn, w_branch_dn, w_out, ffn_norm_g, w_ffn_up, ffn_conv_w, w_ffn_down):
    B = x.shape[0]
    meta = jnp.broadcast_to(meta_tokens[None].astype(x.dtype), (B, N_META, D_MODEL))
    h = jnp.concatenate([meta, x], axis=1)
    L = h.shape[1]
    slopes = jnp.exp2(-ALIBI_MAX_EXP / A_HEADS * jnp.arange(1, A_HEADS + 1, dtype=jnp.float32))
    splits = []
    acc = 0
    for wdt in IN_WIDTHS[:-1]:
        acc += wdt
        splits.append(acc)

    for layer in range(DEPTH):
        lam_init = 0.8 - 0.6 * math.exp(-0.3 * layer)
        u = rms_norm(h, mix_norm_g[layer])
        proj = u @ w_in[layer]
        aq, ak, av, dqkv, dz, db, da, ga, gb = jnp.split(proj, splits, axis=-1)

        aq = rms_norm(aq.reshape(B, L, A_HEADS, 2, A_DH), q_norm_g[layer])
        ak = rms_norm(ak.reshape(B, L, A_HEADS, 2, A_DH), k_norm_g[layer])
        av = av.reshape(B, L, A_HEADS, A_DV)
        lam = (jnp.exp(jnp.sum(lambda_q1[layer].astype(jnp.float32) * lambda_k1[layer].astype(jnp.float32)))
               - jnp.exp(jnp.sum(lambda_q2[layer].astype(jnp.float32) * lambda_k2[layer].astype(jnp.float32)))
               + lam_init)
        ao = diff_attention(aq, ak, av, lam, slopes)
        ao = rms_norm(ao, attn_subln_g[layer]) * (1.0 - lam_init)
        ya = ao.reshape(B, L, A_HEADS * A_DV).astype(h.dtype) @ w_branch_attn[layer]

        dqkv = jax.nn.silu(causal_dwconv(dqkv, dn_conv_w[layer]))
        dq, dk, dv = jnp.split(dqkv, [DN_HEADS * DN_DK, 2 * DN_HEADS * DN_DK], axis=-1)
        dq = l2_normalize(dq.reshape(B, L, DN_HEADS, DN_DK))
        dk = l2_normalize(dk.reshape(B, L, DN_HEADS, DN_DK))
        dv = dv.reshape(B, L, DN_HEADS, DN_DV).astype(jnp.float32)
        beta = jax.nn.sigmoid(db.astype(jnp.float32))
        gdec = -jnp.exp(dn_a_log[layer].astype(jnp.float32)) * jax.nn.softplus(
            da.astype(jnp.float32) + dn_dt_bias[layer].astype(jnp.float32))
        do = gated_delta_rule(dq, dk, dv, beta, gdec)
        do = rms_norm(do, dn_norm_g[layer]) * jax.nn.silu(
            dz.reshape(B, L, DN_HEADS, DN_DV).astype(jnp.float32))
        yb = do.reshape(B, L, DN_HEADS * DN_DV).astype(h.dtype) @ w_branch_dn[layer]

        mixed = jax.nn.sigmoid(ga) * ya + jax.nn.sigmoid(gb) * yb
        h = h + mixed @ w_out[layer]

        f = rms_norm(h, ffn_norm_g[layer]) @ w_ffn_up[layer]
        f = causal_dwconv(f, ffn_conv_w[layer])
        f_gate, f_up = jnp.split(f, 2, axis=-1)
        h = h + (jax.nn.silu(f_gate) * f_up) @ w_ffn_down[layer]

    return h[:, N_META:]
```

```python
import numpy as np
from contextlib import ExitStack
import concourse.bass as bass
import concourse.mybir as mybir
from concourse.bass_utils import run_bass_kernel_spmd

F32 = mybir.dt.float32
BF16 = mybir.dt.bfloat16
AF = mybir.ActivationFunctionType
ALU = mybir.AluOpType
AX = mybir.AxisListType

D = 1024
NC = 8
N_META = 16
PAD = 112
D_FF = 2816
EPS = 1e-6
NEG = -30000.0


class Buf:
    __slots__ = ("name", "w", "r")

    def __init__(self, name=""):
        self.name = name
        self.w = None
        self.r = {}


class Tile(Buf):
    __slots__ = ("t",)

    def __init__(self, name, t):
        Buf.__init__(self, name)
        self.t = t

    def __getitem__(self, k):
        return self.t[k]


class Sched:
    ENGS = ("sync", "scalar", "vector", "tensor", "gpsimd")

    def __init__(self, nc, st, n_dma_sems=24):
        self.nc = nc
        self.st = st
        self.ops = {e: [] for e in self.ENGS}
        self.sem = {}
        self.cnt = {}
        self.seen = {e: {} for e in self.ENGS}
        for e in self.ENGS:
            self.sem[e] = st.enter_context(nc.semaphore("s_" + e))
            self.cnt[e] = 0
        self.dsem = []
        for i in range(n_dma_sems):
            k = "d%d" % i
            self.sem[k] = st.enter_context(nc.semaphore("s_" + k))
            self.cnt[k] = 0
            self.dsem.append(k)
        self.dnext = 0
        self.n_ops = 0

    def sb(self, name, shape, dt):
        return Tile(name, self.st.enter_context(self.nc.sbuf_tensor(name, list(shape), dt)))

    def ps(self, name, shape, dt=F32):
        return Tile(name, self.st.enter_context(self.nc.psum_tensor(name, list(shape), dt)))

    def _deps(self, eng, reads, writes, extra=()):
        need = {}

        def add(tok):
            if tok is None:
                return
            k, v = tok
            if need.get(k, 0) < v:
                need[k] = v
        for b in reads:
            add(b.w)
        for b in writes:
            add(b.w)
            for k, v in b.r.items():
                add((k, v))
        for tok in extra:
            add(tok)
        waits = []
        seen = self.seen[eng]
        for k, v in need.items():
            if k == "tensor" and eng == "tensor":
                continue
            if seen.get(k, 0) >= v:
                continue
            seen[k] = v
            waits.append((k, v))
        return waits

    def _commit(self, tok, reads, writes):
        k, v = tok
        for b in reads:
            if b.r.get(k, 0) < v:
                b.r[k] = v
        for b in writes:
            b.w = tok
            b.r = {}

    def op(self, eng, fn, r=(), w=()):
        waits = self._deps(eng, r, w)
        self.cnt[eng] += 1
        tok = (eng, self.cnt[eng])
        self.ops[eng].append((waits, fn, eng, 1))
        self._commit(tok, r, w)
        self.n_ops += 1
        return tok

    def dma(self, eng, out, in_, r=(), w=()):
        k = self.dsem[self.dnext]
        self.dnext = (self.dnext + 1) % len(self.dsem)
        prev = (k, self.cnt[k]) if self.cnt[k] else None
        waits = self._deps(eng, r, w, extra=(prev,) if prev else ())
        self.cnt[k] += 16
        tok = (k, self.cnt[k])
        self.ops[eng].append((waits, lambda e: e.dma_start(out=out, in_=in_), k, 16))
        self._commit(tok, r, w)
        self.n_ops += 1
        return tok

    def wait_all(self, eng, bufs):
        waits = self._deps(eng, bufs, ())
        self.ops[eng].append((waits, None, None, 0))

    def emit(self):
        nc = self.nc
        block = self.st.enter_context(nc.Block())

        def run(e, lst):
            for waits, fn, k, inc in lst:
                for (sk, v) in waits:
                    e.wait_ge(self.sem[sk], v)
                if fn is not None:
                    fn(e).then_inc(self.sem[k], inc)

        @block.sync
        def _(e):
            run(e, self.ops["sync"])

        @block.scalar
        def _(e):
            run(e, self.ops["scalar"])

        @block.vector
        def _(e):
            run(e, self.ops["vector"])

        @block.tensor
        def _(e):
            run(e, self.ops["tensor"])

        @block.gpsimd
        def _(e):
            run(e, self.ops["gpsimd"])


def mm(S, out, lhsT, rhs, start, stop, r, w):
    return S.op("tensor", lambda e: e.matmul(out, lhsT=lhsT, rhs=rhs, start=start, stop=stop,
                                             skip_group_check=True), r=r, w=w)


def act(S, out, in_, func, r, w, bias=0.0, scale=1.0, accum_out=None):
    if accum_out is None:
        return S.op("scalar", lambda e: e.activation(out=out, in_=in_, func=func, bias=bias,
                                                     scale=scale), r=r, w=w)
    return S.op("scalar", lambda e: e.activation(out=out, in_=in_, func=func, bias=bias,
                                                 scale=scale, accum_out=accum_out), r=r, w=w)


def tt(S, out, in0, in1, op, r, w, eng="vector"):
    return S.op(eng, lambda e: e.tensor_tensor(out=out, in0=in0, in1=in1, op=op), r=r, w=w)


def ts(S, out, in0, s1, s2, op0, op1, r, w, eng="vector"):
    if s2 is None:
        return S.op(eng, lambda e: e.tensor_scalar(out=out, in0=in0, scalar1=s1, scalar2=None,
                                                   op0=op0), r=r, w=w)
    return S.op(eng, lambda e: e.tensor_scalar(out=out, in0=in0, scalar1=s1, scalar2=s2,
                                               op0=op0, op1=op1), r=r, w=w)


def stt(S, out, in0, scalar, in1, op0, op1, r, w):
    return S.op("vector", lambda e: e.scalar_tensor_tensor(out=out, in0=in0, scalar=scalar,
                                                           in1=in1, op0=op0, op1=op1), r=r, w=w)


def cp(S, out, in_, r, w, eng="vector"):
    return S.op(eng, lambda e: e.tensor_copy(out=out, in_=in_), r=r, w=w)


def rstd_from_ss(S, out, ss, scale, eps, tmp, r, w):
    ts(S, tmp, ss, scale, eps, ALU.mult, ALU.add, r=r, w=w)
    act(S, tmp, tmp, AF.Ln, r=w, w=w)
    act(S, out, tmp, AF.Exp, r=w, w=w, scale=-0.5)


def build_B(TB):
    nc = bass.Bass("TRN2", target_bir_lowering=False)
    dr = lambda n, s: nc.dram_tensor(n, list(s), F32, kind="ExternalInput").ap()
    hT = dr("hT", (D, TB))
    aoT = dr("aoT", (D, TB))
    doT = dr("doT", (D, TB))
    wg = dr("wg", (D, 2 * D))
    wA = dr("wA", (D, D))
    wB = dr("wB", (D, D))
    wO = dr("wO", (D, D))
    wU = dr("wU", (D, 2 * D_FF))
    wD = dr("wD", (D_FF, D))
    g1 = dr("g1", (128, 8))
    g2 = dr("g2", (128, 8))
    cw = dr("cw", (128, 3 * 44))
    out = nc.dram_tensor("outT", [D, TB], F32, kind="ExternalOutput").ap()
    hmid = nc.dram_tensor("hmid", [D, TB], F32, kind="Internal").ap()

    NJ = D_FF // 128
    blocks = [(0, 128)]
    t = 128
    while t < TB:
        n = min(256, TB - t)
        blocks.append((t, n))
        t += n

    with ExitStack() as st:
        S = Sched(nc, st)
        ones = S.sb("ones", (128, 128), BF16)
        S.op("gpsimd", lambda e: e.memset(ones[:], 1.0), w=[ones])
        g1s = S.sb("g1s", (128, 8), F32)
        g2s = S.sb("g2s", (128, 8), F32)
        cws = S.sb("cws", (128, 3 * 44), F32)
        S.dma("sync", g1s[:], g1, w=[g1s])
        S.dma("sync", g2s[:], g2, w=[g2s])
        S.dma("sync", cws[:], cw, w=[cws])
        PS = [S.ps("ps%d" % i, (128, 512)) for i in range(8)]
        stg = []
        stg_i = [0]

        def load_weight(dst, dst_view_fn, src, rows, cols, gs=None):
            kc = rows // 128
            srcv = src.rearrange("(c p) n -> p c n", p=128)
            cstep = 2048
            for c in range(kc):
                for c0 in range(0, cols, cstep):
                    n = min(cstep, cols - c0)
                    sg = stg[stg_i[0] % len(stg)]
                    stg_i[0] += 1
                    S.dma("sync", sg[:, 0:n], srcv[:, c, c0:c0 + n], w=[sg])
                    dv = dst_view_fn(c, c0, n)
                    if gs is None:
                        cp(S, dv, sg[:, 0:n], r=[sg], w=[dst], eng="gpsimd")
                    else:
                        ts(S, dv, sg[:, 0:n], gs[:, c:c + 1], None, ALU.mult, None,
                           r=[sg, gs], w=[dst], eng="gpsimd")

        stM = ExitStack()
        S.st = stM
        stg[:] = [S.sb("stg%d" % i, (128, 2048), F32) for i in range(2)]
        hmb = [Buf("hm%d" % i) for i in range(len(blocks))]
        wgs = S.sb("wgs", (128, 8, 2 * D), BF16)
        wAs = S.sb("wAs", (128, 8, D), BF16)
        wBs = S.sb("wBs", (128, 8, D), BF16)
        wOs = S.sb("wOs", (128, 8, D), BF16)
        load_weight(wgs, lambda c, c0, n: wgs[:, c, c0:c0 + n], wg, D, 2 * D, gs=g1s)
        load_weight(wAs, lambda c, c0, n: wAs[:, c, c0:c0 + n], wA, D, D)
        load_weight(wBs, lambda c, c0, n: wBs[:, c, c0:c0 + n], wB, D, D)
        load_weight(wOs, lambda c, c0, n: wOs[:, c, c0:c0 + n], wO, D, D)
        hf = [S.sb("hf%d" % i, (128, 8, 256), F32) for i in range(2)]
        af = [S.sb("af%d" % i, (128, 8, 256), F32) for i in range(2)]
        df = [S.sb("df%d" % i, (128, 8, 256), F32) for i in range(2)]
        hb = S.sb("hb", (128, 8, 256), BF16)
        hsq = S.sb("hsq", (128, 8, 256), BF16)
        ab = S.sb("ab", (128, 8, 256), BF16)
        db = S.sb("db", (128, 8, 256), BF16)
        rstd = S.sb("rstd", (128, 256), F32)
        rtmp = S.sb("rtmp", (128, 256), F32)
        sga = S.sb("sga", (128, 256), F32)
        sgb = S.sb("sgb", (128, 256), F32)
        m1 = S.sb("m1", (128, 256), F32)
        mixed = S.sb("mixed", (128, 8, 256), BF16)
        hTv = hT.rearrange("(c p) t -> p c t", p=128)
        aoTv = aoT.rearrange("(c p) t -> p c t", p=128)
        doTv = doT.rearrange("(c p) t -> p c t", p=128)
        hmv = hmid.rearrange("(c p) t -> p c t", p=128)
        outv = out.rearrange("(c p) t -> p c t", p=128)
        for bi, (t0, n) in enumerate(blocks):
            h_ = hf[bi % 2]
            a_ = af[bi % 2]
            d_ = df[bi % 2]
            S.dma("sync", h_[:, :, 0:n], hTv[:, :, t0:t0 + n], w=[h_])
            S.dma("sync", a_[:, :, 0:n], aoTv[:, :, t0:t0 + n], w=[a_])
            S.dma("sync", d_[:, :, 0:n], doTv[:, :, t0:t0 + n], w=[d_])
            cp(S, hb[:, :, 0:n], h_[:, :, 0:n], r=[h_], w=[hb])
            act(S, hsq[:, :, 0:n], h_[:, :, 0:n], AF.Square, r=[h_], w=[hsq])
            cp(S, ab[:, :, 0:n], a_[:, :, 0:n], r=[a_], w=[ab], eng="gpsimd")
            cp(S, db[:, :, 0:n], d_[:, :, 0:n], r=[d_], w=[db], eng="gpsimd")
            p = PS[0]
            for c in range(8):
                mm(S, p[:, 0:n], ones[:], hsq[:, c, 0:n], c == 0, c == 7, r=[ones, hsq], w=[p])
            rstd_from_ss(S, rstd[:, 0:n], p[:, 0:n], 1.0 / D, EPS, rtmp[:, 0:n], r=[p], w=[rstd, rtmp])
            for m in range(8):
                pgg = PS[1 + (m % 2) * 2]
                pyy = PS[2 + (m % 2) * 2]
                pga, pgb, pya, pyb = pgg, pgg, pyy, pyy
                oga, ogb, oya, oyb = 0, 256, 0, 256
                for c in range(8):
                    mm(S, pga[:, oga:oga + n], wgs[:, c, m * 128:(m + 1) * 128], hb[:, c, 0:n], c == 0,
                       c == 7, r=[wgs, hb], w=[pga])
                for c in range(8):
                    mm(S, pgb[:, ogb:ogb + n], wgs[:, c, D + m * 128:D + (m + 1) * 128], hb[:, c, 0:n],
                       c == 0, c == 7, r=[wgs, hb], w=[pgb])
                for c in range(8):
                    mm(S, pya[:, oya:oya + n], wAs[:, c, m * 128:(m + 1) * 128], ab[:, c, 0:n], c == 0,
                       c == 7, r=[wAs, ab], w=[pya])
                for c in range(8):
                    mm(S, pyb[:, oyb:oyb + n], wBs[:, c, m * 128:(m + 1) * 128], db[:, c, 0:n], c == 0,
                       c == 7, r=[wBs, db], w=[pyb])
                tt(S, sga[:, 0:n], pga[:, oga:oga + n], rstd[:, 0:n], ALU.mult, r=[pga, rstd], w=[sga])
                act(S, sga[:, 0:n], sga[:, 0:n], AF.Sigmoid, r=[sga], w=[sga])
                tt(S, sgb[:, 0:n], pgb[:, ogb:ogb + n], rstd[:, 0:n], ALU.mult, r=[pgb, rstd], w=[sgb])
                act(S, sgb[:, 0:n], sgb[:, 0:n], AF.Sigmoid, r=[sgb], w=[sgb])
                tt(S, m1[:, 0:n], pya[:, oya:oya + n], sga[:, 0:n], ALU.mult, r=[pya, sga], w=[m1])
                tt(S, sgb[:, 0:n], pyb[:, oyb:oyb + n], sgb[:, 0:n], ALU.mult, r=[pyb, sgb], w=[sgb])
                tt(S, mixed[:, m, 0:n], m1[:, 0:n], sgb[:, 0:n], ALU.add, r=[m1, sgb], w=[mixed])
            for m in range(8):
                p = PS[5 + (m % 2)]
                for c in range(8):
                    mm(S, p[:, 0:n], wOs[:, c, m * 128:(m + 1) * 128], mixed[:, c, 0:n], c == 0, c == 7,
                       r=[wOs, mixed], w=[p])
                tt(S, h_[:, m, 0:n], p[:, 0:n], h_[:, m, 0:n], ALU.add, r=[p, h_], w=[h_])
            S.dma("sync", hmv[:, :, t0:t0 + n], h_[:, :, 0:n], r=[h_], w=[hmb[bi]])
        stM.close()
        S.st = st
        bar = Buf("bar")
        toks = []
        for e in ("sync", "scalar", "vector", "tensor", "gpsimd"):
            toks.append((e, S.cnt[e]))
        for k in S.dsem:
            if S.cnt[k]:
                toks.append((k, S.cnt[k]))
        for e in ("sync", "scalar", "vector", "tensor", "gpsimd"):
            waits = []
            for (k, v) in toks:
                if v and S.seen[e].get(k, 0) < v and not (k == e):
                    S.seen[e][k] = v
                    waits.append((k, v))
            S.ops[e].append((waits, None, None, 0))
        wUs = S.sb("wUs", (128, 8, 2 * D_FF), BF16)
        wDs = S.sb("wDs", (128, NJ, D), BF16)
        stg[:] = [S.sb("stgF%d" % i, (128, 2048), F32) for i in range(2)]
        load_weight(wUs, lambda c, c0, n: wUs[:, c, c0:c0 + n], wU, D, 2 * D_FF, gs=g2s)
        load_weight(wDs, lambda c, c0, n: wDs[:, c, c0:c0 + n], wD, D_FF, D)
        hf = [S.sb("hF%d" % i, (128, 8, 256), F32) for i in range(2)]
        hb = S.sb("hbF", (128, 8, 256), BF16)
        hsq = S.sb("hsqF", (128, 8, 256), BF16)
        rstd = S.sb("rstdF", (128, 256), F32)
        rtmp = S.sb("rtmpF", (128, 256), F32)
        actb = S.sb("actb", (128, NJ, 256), BF16)
        fgw = [S.sb("fgw%d" % i, (128, 2 + 256), F32) for i in range(2)]
        fuw = [S.sb("fuw%d" % i, (128, 2 + 256), F32) for i in range(2)]
        hist = S.sb("hist", (128, 2 * NJ, 2), F32)
        cg = S.sb("cg", (128, 256), F32)
        cu = S.sb("cu", (128, 256), F32)
        S.op("vector", lambda e: e.memset(hist[:], 0.0), w=[hist])
        for bi, (t0, n) in enumerate(blocks):
            h_ = hf[bi % 2]
            S.dma("sync", h_[:, :, 0:n], hmv[:, :, t0:t0 + n], r=[hmb[bi]], w=[h_])
            cp(S, hb[:, :, 0:n], h_[:, :, 0:n], r=[h_], w=[hb])
            act(S, hsq[:, :, 0:n], h_[:, :, 0:n], AF.Square, r=[h_], w=[hsq])
            p = PS[0]
            for c in range(8):
                mm(S, p[:, 0:n], ones[:], hsq[:, c, 0:n], c == 0, c == 7, r=[ones, hsq], w=[p])
            rstd_from_ss(S, rstd[:, 0:n], p[:, 0:n], 1.0 / D, EPS, rtmp[:, 0:n], r=[p], w=[rstd, rtmp])
            for j in range(NJ):
                pg = PS[1 + (j % 3) * 2]
                pu = PS[2 + (j % 3) * 2]
                for c in range(8):
                    mm(S, pg[:, 0:n], wUs[:, c, j * 128:(j + 1) * 128], hb[:, c, 0:n], c == 0, c == 7,
                       r=[wUs, hb], w=[pg])
                for c in range(8):
                    mm(S, pu[:, 0:n], wUs[:, c, D_FF + j * 128:D_FF + (j + 1) * 128], hb[:, c, 0:n],
                       c == 0, c == 7, r=[wUs, hb], w=[pu])
                fg = fgw[j % 2]
                fu = fuw[j % 2]
                cp(S, fg[:, 0:2], hist[:, j, :], r=[hist], w=[fg], eng="gpsimd")
                cp(S, fu[:, 0:2], hist[:, NJ + j, :], r=[hist], w=[fu], eng="gpsimd")
                tt(S, fg[:, 2:2 + n], pg[:, 0:n], rstd[:, 0:n], ALU.mult, r=[pg, rstd], w=[fg])
                tt(S, fu[:, 2:2 + n], pu[:, 0:n], rstd[:, 0:n], ALU.mult, r=[pu, rstd], w=[fu])
                for (f_, c_, off) in ((fg, cg, 0), (fu, cu, NJ)):
                    w0 = cws[:, 0 * 44 + off + j:0 * 44 + off + j + 1]
                    w1 = cws[:, 1 * 44 + off + j:1 * 44 + off + j + 1]
                    w2 = cws[:, 2 * 44 + off + j:2 * 44 + off + j + 1]
                    ts(S, c_[:, 0:n], f_[:, 0:n], w0, None, ALU.mult, None, r=[f_, cws], w=[c_])
                    stt(S, c_[:, 0:n], f_[:, 1:1 + n], w1, c_[:, 0:n], ALU.mult, ALU.add,
                        r=[f_, cws, c_], w=[c_])
                    stt(S, c_[:, 0:n], f_[:, 2:2 + n], w2, c_[:, 0:n], ALU.mult, ALU.add,
                        r=[f_, cws, c_], w=[c_])
                    cp(S, hist[:, off + j, :], f_[:, n:n + 2], r=[f_], w=[hist], eng="gpsimd")
                act(S, cg[:, 0:n], cg[:, 0:n], AF.Silu, r=[cg], w=[cg])
                tt(S, actb[:, j, 0:n], cg[:, 0:n], cu[:, 0:n], ALU.mult, r=[cg, cu], w=[actb])
            for m in range(8):
                p = PS[1 + (m % 2)]
                for j in range(NJ):
                    mm(S, p[:, 0:n], wDs[:, j, m * 128:(m + 1) * 128], actb[:, j, 0:n], j == 0,
                       j == NJ - 1, r=[wDs, actb], w=[p])
                tt(S, h_[:, m, 0:n], p[:, 0:n], h_[:, m, 0:n], ALU.add, r=[p, h_], w=[h_])
            ob = Buf("ob")
            S.dma("sync", outv[:, :, t0:t0 + n], h_[:, :, 0:n], r=[h_], w=[ob])
            S.wait_all("sync", [ob])
        S.emit()
    return nc


WA_COLS = 900
C_AQ, C_AK, C_AV, C_DQ, C_DK, C_DV, C_DZ, C_DB = 0, 128, 256, 384, 512, 640, 768, 896
K_ID, K_TRI, K_BLK, K_SELA, K_SELB, K_ML, K_MU, K_MATT, K_RMA, K_RMB, K_SWAP, K_NCONST = \
    0, 128, 256, 384, 512, 640, 768, 896, 1024, 1025, 1026, 1154
P_GQ, P_GK, P_CONV, P_DTB, P_ALOG, P_LAM, P_GSUB, P_GDN, P_LI, P_OML, P_NCOL = 0, 1, 2, 14, 15, 16, 272, 400, 528, 529, 530


def host_consts():
    c = np.zeros((128, K_NCONST), np.float32)
    i = np.arange(128)
    same = (i[:, None] // 64) == (i[None, :] // 64)
    c[:, K_ID:K_ID + 128] = np.eye(128)
    c[:, K_TRI:K_TRI + 128] = ((i[:, None] <= i[None, :]) & same)
    c[:, K_BLK:K_BLK + 128] = same
    c[:, K_SELA:K_SELA + 128] = (i[:, None] < 64)
    c[:, K_SELB:K_SELB + 128] = (i[:, None] >= 64)
    c[:, K_ML:K_ML + 128] = np.where((i[:, None] > i[None, :]) & same, 0.0, NEG)
    c[:, K_MU:K_MU + 128] = np.where((i[None, :] >= i[:, None]) & same, 0.0, NEG)
    c[:, K_MATT:K_MATT + 128] = np.where(i[:, None] > i[None, :], NEG, 0.0)
    c[:, K_SWAP:K_SWAP + 128] = (i[None, :] == (i[:, None] + 64) % 128)
    c[:, K_RMA] = (i < 64)
    c[:, K_RMB] = (i >= 64)
    return c


def build_A(NT, layer, stop=9):
    Lp = NT * 128
    lam_init = 0.8 - 0.6 * float(np.exp(-0.3 * layer))
    nc = bass.Bass("TRN2", target_bir_lowering=False)
    dr = lambda n, s: nc.dram_tensor(n, list(s), F32, kind="ExternalInput").ap()
    hT = dr("hT", (D, Lp))
    wa = dr("wa", (D, WA_COLS))
    g1 = dr("g1", (128, 8))
    kc = dr("kc", (128, K_NCONST))
    pv = dr("pv", (128, P_NCOL))
    aug = dr("aug", (8, Lp))
    aoT = nc.dram_tensor("aoT", [128, Lp], F32, kind="ExternalOutput").ap()
    doT = nc.dram_tensor("doT", [128, Lp], F32, kind="ExternalOutput").ap()
    NB = 256
    blocks = [(0, 1)] + [(t, 2) for t in range(1, NT, 2)]

    with ExitStack() as st:
        S = Sched(nc, st)
        PS = [S.ps("ps%d" % i, (128, 512)) for i in range(8)]
        pool = [5, 6, 7, 0, 1, 2]
        pi = [0]

        def nb():
            b = PS[pool[pi[0] % len(pool)]]
            pi[0] += 1
            return b

        kcs = S.sb("kcs", (128, K_NCONST), F32)
        pvs = S.sb("pvs", (128, P_NCOL), F32)
        g1s = S.sb("g1s", (128, 8), F32)
        S.dma("sync", kcs[:], kc, w=[kcs])
        S.dma("sync", pvs[:], pv, w=[pvs])
        S.dma("sync", g1s[:], g1, w=[g1s])
        identf = kcs[:, K_ID:K_ID + 128]
        trif = kcs[:, K_TRI:K_TRI + 128]
        identb = S.sb("identb", (128, 128), BF16)
        mattb = S.sb("mattb", (128, 128), BF16)
        onesb = S.sb("onesb", (128, 128), BF16)
        onesf = S.sb("onesf", (128, 128), F32)
        negonesf = S.sb("negonesf", (128, 128), F32)
        zerob = S.sb("zerob", (128, 512), BF16)
        blkb = S.sb("blkb", (128, 128), BF16)
        swapb = S.sb("swapb", (128, 128), BF16)
        cp(S, blkb[:], kcs[:, K_BLK:K_BLK + 128], r=[kcs], w=[blkb])
        cp(S, swapb[:], kcs[:, K_SWAP:K_SWAP + 128], r=[kcs], w=[swapb])
        cp(S, identb[:], identf, r=[kcs], w=[identb])
        cp(S, mattb[:], kcs[:, K_MATT:K_MATT + 128], r=[kcs], w=[mattb])
        S.op("gpsimd", lambda e: e.memset(onesb[:], 1.0), w=[onesb])
        S.op("gpsimd", lambda e: e.memset(onesf[:], 1.0), w=[onesf])
        S.op("gpsimd", lambda e: e.memset(negonesf[:], -1.0), w=[negonesf])
        S.op("gpsimd", lambda e: e.memset(zerob[:], 0.0), w=[zerob])
        sc = S.sb("sc", (128, 16), F32)
        lt = S.sb("lt", (128, 128), F32)
        tt(S, lt[:, 0:64], pvs[:, P_LAM:P_LAM + 64], pvs[:, P_LAM + 64:P_LAM + 128], ALU.mult, r=[pvs], w=[lt])
        tt(S, lt[:, 64:128], pvs[:, P_LAM + 128:P_LAM + 192], pvs[:, P_LAM + 192:P_LAM + 256], ALU.mult,
           r=[pvs], w=[lt])
        S.op("vector", lambda e: e.tensor_reduce(out=sc[:, 2:3], in_=lt[:, 0:64], axis=AX.X, op=ALU.add),
             r=[lt], w=[sc])
        S.op("vector", lambda e: e.tensor_reduce(out=sc[:, 3:4], in_=lt[:, 64:128], axis=AX.X, op=ALU.add),
             r=[lt], w=[sc])
        act(S, sc[:, 2:4], sc[:, 2:4], AF.Exp, r=[sc], w=[sc])
        tt(S, sc[:, 0:1], sc[:, 2:3], sc[:, 3:4], ALU.subtract, r=[sc], w=[sc])
        tt(S, sc[:, 0:1], sc[:, 0:1], pvs[:, P_LI:P_LI + 1], ALU.add, r=[sc, pvs], w=[sc])
        act(S, sc[:, 1:2], pvs[:, P_ALOG:P_ALOG + 1], AF.Exp, r=[pvs], w=[sc])
        ts(S, sc[:, 1:2], sc[:, 1:2], -1.0, None, ALU.mult, None, r=[sc], w=[sc])
        S.op("vector", lambda e: e.memset(sc[:, 8:9], 1.0), w=[sc])
        S.op("vector", lambda e: e.memset(sc[:, 9:10], -8.0), w=[sc])
        one_col = sc[:, 8:9]
        m8_col = sc[:, 9:10]
        lam_col = sc[:, 0:1]
        negA_col = sc[:, 1:2]

        wst = S.sb("wst", (128, WA_COLS), F32)
        Wb = S.sb("Wb", (128, 8, WA_COLS), BF16)
        wav = wa.rearrange("(c p) n -> p c n", p=128)
        for c in range(8):
            S.dma("sync", wst[:], wav[:, c, :], w=[wst])
            ts(S, Wb[:, c, :], wst[:], g1s[:, c:c + 1], None, ALU.mult, None, r=[wst, g1s], w=[Wb])

        KA = [S.sb("KA%d" % m, (128, Lp), BF16) for m in range(2)]
        QA = [S.sb("QA%d" % m, (128, NB), BF16) for m in range(2)]
        for m in range(2):
            S.op("vector", lambda e, m=m: e.memset(KA[m][64:128, :], 0.0), w=[KA[m]])
            S.op("vector", lambda e, m=m: e.memset(QA[m][64:128, :], 0.0), w=[QA[m]])
        VA = S.sb("VA", (128, NT, 130), BF16)
        S.op("vector", lambda e: e.memset(VA[:, :, 128:130], 0.0), w=[VA])
        S.op("vector", lambda e: e.memset(VA[:, :, 128:129], 1.0), w=[VA])
        S.op("vector", lambda e: e.memset(VA[0:PAD, 0, 128:129], 0.0), w=[VA])
        augs = S.sb("augs", (68, 2, NB), F32)
        Sb = [S.sb("Sb%d" % i, (128, 128), BF16) for i in range(2)]
        S.op("gpsimd", lambda e: e.memset(Sb[0][:], 0.0), w=[Sb[0]])
        s_cur = [0]
        rawq = S.sb("rawq", (128, 3 + NB), F32)
        rawk = S.sb("rawk", (128, 3 + NB), F32)
        rawv = S.sb("rawv", (128, 3 + NB), F32)
        for r_ in (rawq, rawk, rawv):
            S.op("gpsimd", lambda e, r_=r_: e.memset(r_[:, 0:3], 0.0), w=[r_])

        xf = [S.sb("xf%d" % i, (128, 8, NB), F32) for i in range(1)]
        xb = S.sb("xb", (128, 8, NB), BF16)
        xsq = S.sb("xsq", (128, 8, NB), BF16)
        rstd = S.sb("rstd", (128, NB), F32)
        rtmp = S.sb("rtmp", (128, NB), F32)
        rcol = S.sb("rcol", (128, 2), F32)
        qf = S.sb("qf", (128, NB), F32)
        qsq = S.sb("qsq", (128, NB), BF16)
        qr = S.sb("qr", (128, NB), F32)
        qt2 = S.sb("qt2", (128, NB), F32)
        qnx = S.sb("qnx", (128, NB), BF16)
        cq = S.sb("cq", (128, NB), F32)
        ck = S.sb("ck", (128, NB), F32)
        cv = S.sb("cv", (128, NB), F32)
        sgm = S.sb("sgm", (128, NB), F32)
        dsq = S.sb("dsq", (128, 2, NB), BF16)
        drn = S.sb("drn", (128, 2, NB), F32)
        dtmp = S.sb("dtmp", (128, 2, NB), F32)
        qn = S.sb("qn", (128, NB), F32)
        kn = S.sb("kn", (128, NB), F32)
        qnb = S.sb("qnb", (128, NB), BF16)
        knb = S.sb("knb", (128, NB), BF16)
        PT = [S.sb("PT%d" % i, (128, 512), BF16) for i in range(2)]
        aoblk = S.sb("aoblk", (128, NB), F32)
        doblk = S.sb("doblk", (128, NB), F32)
        hTv = hT.rearrange("(c p) t -> p c t", p=128)

        class TW:
            pass
        tw = []
        for i in range(2):
            w_ = TW()
            for nm, shp, dt in (("bd", (128, 2), F32), ("sm", (128, 16), F32), ("gs", (128, 4), F32),
                                ("trig", (128, 128), F32), ("decay", (128, 128), F32),
                                ("decayT", (128, 128), F32), ("egb", (128, 128), F32),
                                ("A", (128, 128), F32), ("AT", (128, 128), F32),
                                ("P", (128, 128), F32), ("PTt", (128, 128), F32),
                                ("P2", (128, 128), F32), ("PT2", (128, 128), F32),
                                ("x", (128, 256), F32), ("qkT", (128, 128), BF16),
                                ("kdA", (128, 128), BF16), ("kdB", (128, 128), BF16),
                                ("ub", (128, 128), BF16), ("wb", (128, 128), BF16),
                                ("qdT", (128, 128), F32), ("QpT", (128, 128), BF16),
                                ("O0", (128, 128), F32), ("MTA", (128, 128), BF16), ("MTB", (128, 128), BF16),
                                ("NA", (128, 128), F32), ("NBm", (128, 128), F32),
                                ("o", (128, 128), F32), ("zs", (128, 128), F32), ("junk", (128, 128), F32),
                                ("dg", (128, 128), F32), ("aot", (128, 128), F32), ("t1", (128, 128), F32),
                                ("aon", (128, 128), F32)):
                setattr(w_, nm, S.sb("%s_%d" % (nm, i), shp, dt))
            tw.append(w_)

        def small_rstd(out, ss, scale, tmp, r, w):
            ts(S, tmp, ss, scale, EPS, ALU.mult, ALU.add, r=r, w=w)
            act(S, tmp, tmp, AF.Ln, r=w, w=w)
            act(S, out, tmp, AF.Exp, r=w, w=w, scale=-0.5)

        for bi, (tl0, ntl) in enumerate(blocks):
            if stop <= 0.1:
                continue
            t0 = tl0 * 128
            n = ntl * 128
            x_ = xf[0]
            S.dma("sync", x_[:, :, 0:n], hTv[:, :, t0:t0 + n], w=[x_])
            S.dma("sync", augs[64:68, 0, 0:n], aug[0:4, t0:t0 + n], w=[augs])
            S.dma("sync", augs[64:68, 1, 0:n], aug[4:8, t0:t0 + n], w=[augs])
            cp(S, xb[:, :, 0:n], x_[:, :, 0:n], r=[x_], w=[xb])
            act(S, xsq[:, :, 0:n], x_[:, :, 0:n], AF.Square, r=[x_], w=[xsq])
            p = nb()
            for c in range(8):
                mm(S, p[:, 0:n], onesb[:], xsq[:, c, 0:n], c == 0, c == 7, r=[onesb, xsq], w=[p])
            rstd_from_ss(S, rstd[:, 0:n], p[:, 0:n], 1.0 / D, EPS, rtmp[:, 0:n], r=[p], w=[rstd, rtmp])
            p = nb()
            for i in range(ntl):
                S.op("tensor", lambda e, p=p, i=i: e.transpose(out=p[:, i * 128:(i + 1) * 128],
                                                               in_=rstd[:, i * 128:(i + 1) * 128],
                                                               identity=identf), r=[rstd, kcs], w=[p])
            for i in range(ntl):
                cp(S, rcol[:, i:i + 1], p[:, i * 128:i * 128 + 1], r=[p], w=[rcol])
            for m in range(2):
                cp(S, QA[m][64:68, 0:n], augs[64:68, 0, 0:n], r=[augs], w=[QA[m]])
                cp(S, KA[m][64:68, t0:t0 + n], augs[64:68, 1, 0:n], r=[augs], w=[KA[m]])
            if stop <= 0.2:
                continue
            for which, c0 in (("q", C_AQ), ("k", C_AK)):
                p = nb()
                for c in range(8):
                    mm(S, p[:, 0:n], Wb[:, c, c0:c0 + 128], xb[:, c, 0:n], c == 0, c == 7, r=[Wb, xb], w=[p])
                tt(S, qf[:, 0:n], p[:, 0:n], rstd[:, 0:n], ALU.mult, r=[p, rstd], w=[qf])
                act(S, qsq[:, 0:n], qf[:, 0:n], AF.Square, r=[qf], w=[qsq])
                p2 = nb()
                mm(S, p2[:, 0:n], blkb[:], qsq[:, 0:n], True, True, r=[blkb, qsq], w=[p2])
                rstd_from_ss(S, qr[:, 0:n], p2[:, 0:n], 1.0 / 64, EPS, qt2[:, 0:n], r=[p2], w=[qr, qt2])
                tt(S, qt2[:, 0:n], qf[:, 0:n], qr[:, 0:n], ALU.mult, r=[qf, qr], w=[qt2])
                if which == "q":
                    ts(S, qnx[:, 0:n], qt2[:, 0:n], pvs[:, P_GQ:P_GQ + 1], 0.125, ALU.mult, ALU.mult,
                       r=[qt2, pvs], w=[qnx])
                    dst = [QA[0][0:64, 0:n], QA[1][0:64, 0:n]]
                    dt_ = QA
                else:
                    ts(S, qnx[:, 0:n], qt2[:, 0:n], pvs[:, P_GK:P_GK + 1], None, ALU.mult, None,
                       r=[qt2, pvs], w=[qnx])
                    dst = [KA[0][0:64, t0:t0 + n], KA[1][0:64, t0:t0 + n]]
                    dt_ = KA
                cp(S, dst[0], qnx[0:64, 0:n], r=[qnx], w=[dt_[0]])
                p3 = nb()
                mm(S, p3[:, 0:n], swapb[:], qnx[:, 0:n], True, True, r=[swapb, qnx], w=[p3])
                cp(S, dst[1], p3[0:64, 0:n], r=[p3], w=[dt_[1]])
            if stop <= 0.3:
                continue
            for i in range(ntl):
                T = tw[i]
                tile = tl0 + i
                cs = slice(i * 128, (i + 1) * 128)
                p = nb()
                for c in range(8):
                    mm(S, p[:, 0:128], xb[:, c, cs], Wb[:, c, C_AV:C_AV + 128], c == 0, c == 7,
                       r=[xb, Wb], w=[p])
                for c in range(8):
                    mm(S, p[:, 128:256], xb[:, c, cs], Wb[:, c, C_DZ:C_DZ + 128], c == 0, c == 7,
                       r=[xb, Wb], w=[p])
                for c in range(8):
                    mm(S, p[:, 256:260], xb[:, c, cs], Wb[:, c, C_DB:C_DB + 4], c == 0, c == 7,
                       r=[xb, Wb], w=[p])
                if stop <= 0.33:
                    continue
                ts(S, VA[:, tile, 0:128], p[:, 0:128], rcol[:, i:i + 1], None, ALU.mult, None,
                   r=[p, rcol], w=[VA])
                if stop <= 0.34:
                    continue
                ts(S, T.zs[:], p[:, 128:256], rcol[:, i:i + 1], None, ALU.mult, None, r=[p, rcol], w=[T.zs])
                act(S, T.junk[:], T.zs[:], AF.Sigmoid, r=[T.zs], w=[T.junk])
                tt(S, T.zs[:], T.zs[:], T.junk[:], ALU.mult, r=[T.zs, T.junk], w=[T.zs])
                if stop <= 0.345:
                    continue
                ts(S, T.bd[:], p[:, 256:258], rcol[:, i:i + 1], None, ALU.mult, None, r=[p, rcol], w=[T.bd])
                if stop <= 0.35:
                    continue
                act(S, T.sm[:, 0:1], T.bd[:, 0:1], AF.Sigmoid, r=[T.bd], w=[T.sm])
                tt(S, T.sm[:, 1:2], T.bd[:, 1:2], pvs[:, P_DTB:P_DTB + 1], ALU.add, r=[T.bd, pvs], w=[T.sm])
                act(S, T.sm[:, 1:2], T.sm[:, 1:2], AF.Exp, r=[T.sm], w=[T.sm])
                ts(S, T.sm[:, 1:2], T.sm[:, 1:2], 1.0, None, ALU.add, None, r=[T.sm], w=[T.sm])
                act(S, T.sm[:, 1:2], T.sm[:, 1:2], AF.Ln, r=[T.sm], w=[T.sm])
                tt(S, T.sm[:, 1:2], T.sm[:, 1:2], negA_col, ALU.mult, r=[T.sm, sc], w=[T.sm])
            if stop <= 1:
                continue
            for (raw, cdst, c0, ti) in ((rawq, cq, C_DQ, 0), (rawk, ck, C_DK, 1), (rawv, cv, C_DV, 2)):
                p = nb()
                for c in range(8):
                    mm(S, p[:, 0:n], Wb[:, c, c0:c0 + 128], xb[:, c, 0:n], c == 0, c == 7, r=[Wb, xb], w=[p])
                tt(S, raw[:, 3:3 + n], p[:, 0:n], rstd[:, 0:n], ALU.mult, r=[p, rstd], w=[raw])
                tap = lambda j: pvs[:, P_CONV + ti * 4 + j:P_CONV + ti * 4 + j + 1]
                ts(S, cdst[:, 0:n], raw[:, 0:n], tap(0), None, ALU.mult, None, r=[raw, pvs], w=[cdst])
                for j in (1, 2, 3):
                    stt(S, cdst[:, 0:n], raw[:, j:j + n], tap(j), cdst[:, 0:n], ALU.mult, ALU.add,
                        r=[raw, pvs, cdst], w=[cdst])
                cp(S, raw[:, 0:3], raw[:, n:n + 3], r=[raw], w=[raw], eng="gpsimd")
                act(S, sgm[:, 0:n], cdst[:, 0:n], AF.Sigmoid, r=[cdst], w=[sgm])
                tt(S, cdst[:, 0:n], cdst[:, 0:n], sgm[:, 0:n], ALU.mult, r=[cdst, sgm], w=[cdst])
            act(S, dsq[:, 0, 0:n], cq[:, 0:n], AF.Square, r=[cq], w=[dsq])
            act(S, dsq[:, 1, 0:n], ck[:, 0:n], AF.Square, r=[ck], w=[dsq])
            p = nb()
            for m in range(2):
                mm(S, p[:, m * 256:m * 256 + n], onesb[:], dsq[:, m, 0:n], True, True, r=[onesb, dsq], w=[p])
            for m in range(2):
                rstd_from_ss(S, drn[:, m, 0:n], p[:, m * 256:m * 256 + n], 1.0, 1e-6, dtmp[:, m, 0:n],
                             r=[p], w=[drn, dtmp])
            stt(S, qn[:, 0:n], cq[:, 0:n], 128.0 ** -0.5, drn[:, 0, 0:n], ALU.mult, ALU.mult,
                r=[cq, drn], w=[qn])
            tt(S, kn[:, 0:n], ck[:, 0:n], drn[:, 1, 0:n], ALU.mult, r=[ck, drn], w=[kn])
            cp(S, qnb[:, 0:n], qn[:, 0:n], r=[qn], w=[qnb], eng="gpsimd")
            cp(S, knb[:, 0:n], kn[:, 0:n], r=[kn], w=[knb], eng="gpsimd")

            if stop <= 2:
                continue
            tiles = list(range(ntl))
            csl = lambda i: slice(i * 128, (i + 1) * 128)
            for i in tiles:
                T = tw[i]
                beta = T.sm[:, 0:1]
                g = T.sm[:, 1:2]
                ts(S, T.trig[:], trif, g, None, ALU.mult, None, r=[kcs, T.sm], w=[T.trig])
                p = nb()
                for q, kk in enumerate((K_TRI, K_BLK, K_SELA, K_SELB)):
                    mm(S, p[:, q:q + 1], kcs[:, kk:kk + 128], g, True, True, r=[kcs, T.sm], w=[p])
                cp(S, T.gs[:], p[:, 0:4], r=[p], w=[T.gs])
                cp(S, T.sm[:, 2:3], T.gs[:, 0:1], r=[T.gs], w=[T.sm])
                tt(S, T.sm[:, 3:4], T.gs[:, 1:2], T.gs[:, 0:1], ALU.subtract, r=[T.gs], w=[T.sm])
                cp(S, T.sm[:, 4:6], T.gs[:, 2:4], r=[T.gs], w=[T.sm])
                act(S, T.sm[:, 6:10], T.sm[:, 2:6], AF.Exp, r=[T.sm], w=[T.sm])
                ts(S, T.sm[:, 10:11], T.gs[:, 0:1], -1.0, None, ALU.mult, None, r=[T.gs], w=[T.sm])
                tt(S, T.sm[:, 11:12], T.sm[:, 0:1], T.sm[:, 6:7], ALU.mult, r=[T.sm], w=[T.sm])
                tt(S, T.sm[:, 12:13], T.sm[:, 7:8], kcs[:, K_RMA:K_RMA + 1], ALU.mult, r=[T.sm, kcs], w=[T.sm])
                tt(S, T.sm[:, 13:14], T.sm[:, 7:8], kcs[:, K_RMB:K_RMB + 1], ALU.mult, r=[T.sm, kcs], w=[T.sm])
            for i in tiles:
                T = tw[i]
                Gc, nG = T.sm[:, 2:3], T.sm[:, 10:11]
                pL = nb()
                mm(S, pL[:, 0:128], negonesf[:], T.trig[:], True, False, r=[negonesf, T.trig], w=[pL])
                mm(S, pL[:, 0:128], identf, kcs[:, K_ML:K_ML + 128], False, True, r=[kcs], w=[pL])
                mm(S, pL[:, 128:256], onesf[:], T.trig[:], True, False, r=[onesf, T.trig], w=[pL])
                mm(S, pL[:, 128:256], identf, kcs[:, K_MU:K_MU + 128], False, True, r=[kcs], w=[pL])
                mm(S, pL[:, 256:384], onesf[:], T.trig[:], True, True, r=[onesf, T.trig], w=[pL])
                ts(S, T.decay[:], pL[:, 0:128], Gc, None, ALU.add, None, r=[pL, T.sm], w=[T.decay])
                act(S, T.decay[:], T.decay[:], AF.Exp, r=[T.decay], w=[T.decay])
                ts(S, T.decayT[:], pL[:, 128:256], nG, None, ALU.add, None, r=[pL, T.sm], w=[T.decayT])
                act(S, T.decayT[:], T.decayT[:], AF.Exp, r=[T.decayT], w=[T.decayT])
                act(S, T.egb[:], pL[:, 256:384], AF.Exp, r=[pL], w=[T.egb])
            for i in tiles:
                T = tw[i]
                beta = T.sm[:, 0:1]
                pg = nb()
                mm(S, pg[:, 0:128], knb[:, csl(i)], knb[:, csl(i)], True, True, r=[knb], w=[pg])
                mm(S, pg[:, 128:256], knb[:, csl(i)], qnb[:, csl(i)], True, True, r=[knb, qnb], w=[pg])
                S.op("tensor", lambda e, pg=pg, i=i: e.transpose(out=pg[:, 256:384], in_=cv[:, csl(i)],
                                                                 identity=identf), r=[cv, kcs], w=[pg])
                S.op("tensor", lambda e, pg=pg, i=i: e.transpose(out=pg[:, 384:512], in_=kn[:, csl(i)],
                                                                 identity=identf), r=[kn, kcs], w=[pg])
                stt(S, T.A[:], pg[:, 0:128], beta, T.decay[:], ALU.mult, ALU.mult, r=[pg, T.sm, T.decay], w=[T.A])
                tt(S, T.qkT[:], pg[:, 128:256], T.decayT[:], ALU.mult, r=[pg, T.decayT], w=[T.qkT])
                ts(S, T.x[:, 0:128], pg[:, 256:384], beta, None, ALU.mult, None, r=[pg, T.sm], w=[T.x])
                ts(S, T.x[:, 128:256], pg[:, 384:512], T.sm[:, 11:12], None, ALU.mult, None, r=[pg, T.sm], w=[T.x])
                ts(S, T.kdA[:], pg[:, 384:512], T.sm[:, 12:13], None, ALU.mult, None, r=[pg, T.sm], w=[T.kdA])
                ts(S, T.kdB[:], pg[:, 384:512], T.sm[:, 13:14], None, ALU.mult, None, r=[pg, T.sm], w=[T.kdB])
                tt(S, T.qdT[:], qn[:, csl(i)], T.egb[:], ALU.mult, r=[qn, T.egb], w=[T.qdT])
            for i in tiles:
                T = tw[i]
                p = nb()
                S.op("tensor", lambda e, p=p, T=T: e.transpose(out=p[:, 0:128], in_=T.A[:], identity=identf),
                     r=[T.A, kcs], w=[p])
                cp(S, T.AT[:], p[:, 0:128], r=[p], w=[T.AT])
            for i in tiles:
                T = tw[i]
                p = nb()
                mm(S, p[:, 0:256], T.AT[:], T.x[:], True, True, r=[T.AT, T.x], w=[p])
                tt(S, T.x[:], T.x[:], p[:, 0:256], ALU.subtract, r=[T.x, p], w=[T.x])
            cur = {i: (tw[i].A, tw[i].AT) for i in tiles}
            for k in range(1, 6):
                for i in tiles:
                    T = tw[i]
                    Pm, PTm = cur[i]
                    nP, nPT = (T.P, T.PTt) if k % 2 == 1 else (T.P2, T.PT2)
                    p = nb()
                    mm(S, p[:, 0:128], Pm[:], PTm[:], True, True, r=[Pm, PTm], w=[p])
                    cp(S, nPT[:], p[:, 0:128], r=[p], w=[nPT])
                    if k < 5:
                        mm(S, p[:, 128:256], PTm[:], Pm[:], True, True, r=[Pm, PTm], w=[p])
                        cp(S, nP[:], p[:, 128:256], r=[p], w=[nP])
                    cur[i] = (nP, nPT)
                for i in tiles:
                    T = tw[i]
                    p = nb()
                    mm(S, p[:, 0:256], cur[i][1][:], T.x[:], True, True, r=[cur[i][1], T.x], w=[p])
                    tt(S, T.x[:], T.x[:], p[:, 0:256], ALU.add, r=[T.x, p], w=[T.x])
            for i in tiles:
                T = tw[i]
                cp(S, T.ub[:], T.x[:, 0:128], r=[T.x], w=[T.ub], eng="gpsimd")
                cp(S, T.wb[:], T.x[:, 128:256], r=[T.x], w=[T.wb], eng="gpsimd")
            for i in tiles:
                T = tw[i]
                p = nb()
                mm(S, p[:, 0:128], T.wb[:], T.qkT[:], True, True, r=[T.wb, T.qkT], w=[p])
                mm(S, p[:, 128:256], T.qkT[:], T.ub[:], True, True, r=[T.qkT, T.ub], w=[p])
                tt(S, T.QpT[:], T.qdT[:], p[:, 0:128], ALU.subtract, r=[T.qdT, p], w=[T.QpT])
                cp(S, T.O0[:], p[:, 128:256], r=[p], w=[T.O0])
                p = nb()
                mm(S, p[:, 0:128], T.wb[:], T.kdA[:], True, True, r=[T.wb, T.kdA], w=[p])
                mm(S, p[:, 128:256], T.wb[:], T.kdB[:], True, True, r=[T.wb, T.kdB], w=[p])
                mm(S, p[:, 256:384], T.kdA[:], T.ub[:], True, True, r=[T.kdA, T.ub], w=[p])
                mm(S, p[:, 384:512], T.kdB[:], T.ub[:], True, True, r=[T.kdB, T.ub], w=[p])
                stt(S, T.MTA[:], identf, T.sm[:, 8:9], p[:, 0:128], ALU.mult, ALU.subtract,
                    r=[kcs, T.sm, p], w=[T.MTA])
                stt(S, T.MTB[:], identf, T.sm[:, 9:10], p[:, 128:256], ALU.mult, ALU.subtract,
                    r=[kcs, T.sm, p], w=[T.MTB])
                cp(S, T.NA[:], p[:, 256:384], r=[p], w=[T.NA])
                cp(S, T.NBm[:], p[:, 384:512], r=[p], w=[T.NBm])
            if stop <= 3:
                continue
            for i in tiles:
                T = tw[i]
                for (rows, MT, Nm) in ((slice(0, 64), T.MTA, T.NA), (slice(64, 128), T.MTB, T.NBm)):
                    Sc = Sb[s_cur[0] % 2]
                    Sn = Sb[(s_cur[0] + 1) % 2]
                    s_cur[0] += 1
                    p = nb()
                    mm(S, p[:, 0:128], T.QpT[:], Sc[:], True, True, r=[T.QpT, Sc], w=[p])
                    mm(S, p[:, 128:256], MT[:], Sc[:], True, True, r=[MT, Sc], w=[p])
                    tt(S, Sn[:], p[:, 128:256], Nm[:], ALU.add, r=[p, Nm], w=[Sn])
                    tt(S, T.o[rows, :], p[rows, 0:128], T.O0[rows, :], ALU.add, r=[p, T.O0], w=[T.o])
                tt(S, T.junk[:], T.o[:], T.o[:], ALU.mult, r=[T.o], w=[T.junk])
                S.op("vector", lambda e, T=T: e.tensor_reduce(out=T.sm[:, 14:15], in_=T.junk[:], axis=AX.X,
                                                              op=ALU.add), r=[T.junk], w=[T.sm])
                small_rstd(T.sm[:, 15:16], T.sm[:, 14:15], 1.0 / 128, T.sm[:, 14:15], r=[T.sm], w=[T.sm])
                stt(S, T.dg[:], T.o[:], T.sm[:, 15:16], pvs[:, P_GDN:P_GDN + 128], ALU.mult, ALU.mult,
                    r=[T.o, T.sm, pvs], w=[T.dg])
                tt(S, T.dg[:], T.dg[:], T.zs[:], ALU.mult, r=[T.dg, T.zs], w=[T.dg])
                p = nb()
                S.op("tensor", lambda e, p=p, T=T: e.transpose(out=p[:, 0:128], in_=T.dg[:], identity=identf),
                     r=[T.dg, kcs], w=[p])
                cp(S, doblk[:, csl(i)], p[:, 0:128], r=[p], w=[doblk])
            S.dma("sync", doT[:, t0:t0 + n], doblk[:, 0:n], r=[doblk], w=[])

            if stop <= 4:
                continue
            accs = [PS[3], PS[4]]
            for ch in range(ntl):
                mm(S, accs[ch][:, :], zerob[:, 0:128], zerob[:, :], True, True, r=[zerob], w=[accs[ch]])
            last_t = tl0 + ntl - 1
            for kt in range(0, last_t + 1):
                ST = PS[kt % 3]
                PTt = PT[kt % 2]
                c_lo = 0 if kt <= tl0 else 128
                dch = kt - tl0 if kt >= tl0 else None
                for m in range(2):
                    mm(S, ST[:, m * 256 + c_lo:m * 256 + n], KA[m][:, kt * 128:(kt + 1) * 128],
                       QA[m][:, c_lo:n], True, dch is None, r=[KA[m], QA[m]], w=[ST])
                    if dch is not None:
                        mm(S, ST[:, m * 256 + dch * 128:m * 256 + (dch + 1) * 128], identb[:], mattb[:],
                           False, True, r=[identb, mattb], w=[ST])
                sv = ST[:, :].rearrange("p (m c) -> p m c", m=2)[:, :, c_lo:n]
                pvw = PTt[:, :].rearrange("p (m c) -> p m c", m=2)[:, :, c_lo:n]
                act(S, pvw, sv, AF.Exp, r=[ST], w=[PTt])
                for ch in range(c_lo // 128, ntl):
                    for m in range(2):
                        mm(S, accs[ch][:, m * 130:(m + 1) * 130],
                           PTt[:, m * 256 + ch * 128:m * 256 + (ch + 1) * 128], VA[:, kt, 0:130], False,
                           kt == tl0 + ch, r=[PTt, VA], w=[accs[ch]])
            for ch in range(ntl):
                T = tw[ch]
                acc = accs[ch]
                den = acc[:, 0:260].rearrange("p (a b) -> p a b", b=130)[:, :, 128:129]
                S.op("vector", lambda e, den=den, T=T: e.tensor_scalar(
                    out=T.sm[:, 2:4].rearrange("p (a b) -> p a b", b=1), in0=den, scalar1=1e-30, scalar2=None,
                    op0=ALU.max), r=[acc], w=[T.sm])
                S.op("vector", lambda e, T=T: e.reciprocal(out=T.sm[:, 4:6], in_=T.sm[:, 2:4]), r=[T.sm], w=[T.sm])
                tt(S, T.sm[:, 5:6], T.sm[:, 5:6], lam_col, ALU.mult, r=[T.sm, sc], w=[T.sm])
                ts(S, T.t1[:], acc[:, 130:258], T.sm[:, 5:6], None, ALU.mult, None, r=[acc, T.sm], w=[T.t1])
                stt(S, T.aot[:], acc[:, 0:128], T.sm[:, 4:5], T.t1[:], ALU.mult, ALU.subtract,
                    r=[acc, T.sm, T.t1], w=[T.aot])
                tt(S, T.junk[:], T.aot[:], T.aot[:], ALU.mult, r=[T.aot], w=[T.junk])
                S.op("vector", lambda e, T=T: e.tensor_reduce(out=T.sm[:, 6:7], in_=T.junk[:], axis=AX.X,
                                                              op=ALU.add), r=[T.junk], w=[T.sm])
                small_rstd(T.sm[:, 7:8], T.sm[:, 6:7], 1.0 / 128, T.sm[:, 6:7], r=[T.sm], w=[T.sm])
                tt(S, T.sm[:, 7:8], T.sm[:, 7:8], pvs[:, P_OML:P_OML + 1], ALU.mult, r=[T.sm, pvs], w=[T.sm])
                stt(S, T.aon[:], T.aot[:], T.sm[:, 7:8], pvs[:, P_GSUB:P_GSUB + 128], ALU.mult, ALU.mult,
                    r=[T.aot, T.sm, pvs], w=[T.aon])
                p = nb()
                S.op("tensor", lambda e, p=p, T=T: e.transpose(out=p[:, 0:128], in_=T.aon[:], identity=identf),
                     r=[T.aon, kcs], w=[p])
                cp(S, aoblk[:, ch * 128:(ch + 1) * 128], p[:, 0:128], r=[p], w=[aoblk])
            S.dma("sync", aoT[:, t0:t0 + n], aoblk[:, 0:n], r=[aoblk], w=[])
        S.ops["sync"].append(([(k, S.cnt[k]) for k in S.dsem if S.cnt[k]], None, None, 0))
        S.emit()
    return nc


def host_aug(head, Lp):
    slope = 2.0 ** (-(head + 1))
    t = np.arange(Lp)
    qt = (t // 128).astype(np.float32)
    tr = (t % 128).astype(np.float32)
    a = np.ones((8, Lp), np.float32)
    a[0] = -slope * 128.0 * qt
    a[1] = -slope * tr
    a[6] = slope * 128.0 * qt
    a[7] = slope * tr
    return a


def host_A_inputs(hT, p, l, head, kc):
    Lp = hT.shape[1]
    w = p["w_in"][l]
    h = head
    cols = [w[:, h * 128:(h + 1) * 128], w[:, 1024 + h * 128:1024 + (h + 1) * 128],
            w[:, 2048 + h * 128:2048 + (h + 1) * 128],
            w[:, 3072 + h * 128:3072 + (h + 1) * 128], w[:, 4096 + h * 128:4096 + (h + 1) * 128],
            w[:, 5120 + h * 128:5120 + (h + 1) * 128], w[:, 6144 + h * 128:6144 + (h + 1) * 128],
            w[:, 7168 + h:7169 + h], w[:, 7176 + h:7177 + h], np.zeros((D, 2), np.float32)]
    wa = np.ascontiguousarray(np.concatenate(cols, axis=1))
    pv = np.zeros((128, P_NCOL), np.float32)
    pv[:, P_GQ] = np.tile(p["q_norm_g"][l], 2)
    pv[:, P_GK] = np.tile(p["k_norm_g"][l], 2)
    cwt = p["dn_conv_w"][l]
    for ti in range(3):
        for j in range(4):
            pv[:, P_CONV + ti * 4 + j] = cwt[j, ti * 1024 + h * 128:ti * 1024 + (h + 1) * 128]
    pv[:, P_DTB] = p["dn_dt_bias"][l][h]
    pv[:, P_ALOG] = p["dn_a_log"][l][h]
    lam4 = np.concatenate([p["lambda_q1"][l], p["lambda_k1"][l], p["lambda_q2"][l], p["lambda_k2"][l]])
    pv[:, P_LAM:P_LAM + 256] = lam4[None, :]
    pv[:, P_GSUB:P_GSUB + 128] = p["attn_subln_g"][l][None, :]
    pv[:, P_GDN:P_GDN + 128] = p["dn_norm_g"][l][None, :]
    lam_init = 0.8 - 0.6 * float(np.exp(-0.3 * l))
    pv[:, P_LI] = lam_init
    pv[:, P_OML] = 1.0 - lam_init
    return {"hT": hT, "wa": wa, "g1": np.ascontiguousarray(p["mix_norm_g"][l].reshape(8, 128).T),
            "kc": kc, "pv": pv, "aug": host_aug(h, Lp)}


def host_B_inputs(hT, aoT, doT, p, l):
    w = p["w_in"][l]
    cw = p["ffn_conv_w"][l]
    cwl = np.ascontiguousarray(cw.reshape(3, 44, 128).transpose(2, 0, 1).reshape(128, 132))
    return {
        "hT": np.ascontiguousarray(hT), "aoT": np.ascontiguousarray(aoT), "doT": np.ascontiguousarray(doT),
        "wg": np.ascontiguousarray(w[:, 7184:]), "wA": p["w_branch_attn"][l], "wB": p["w_branch_dn"][l],
        "wO": p["w_out"][l], "wU": p["w_ffn_up"][l], "wD": p["w_ffn_down"][l],
        "g1": np.ascontiguousarray(p["mix_norm_g"][l].reshape(8, 128).T),
        "g2": np.ascontiguousarray(p["ffn_norm_g"][l].reshape(8, 128).T),
        "cw": cwl,
    }


_PROGS = {}


def _prog(key, fn):
    if key not in _PROGS:
        _PROGS[key] = fn()
    return _PROGS[key]


def kernel(**inputs):
    p = {k: np.asarray(v, dtype=np.float32) for k, v in inputs.items()}
    x = p["x"]
    SEQ = x.shape[1]
    L = SEQ + N_META
    Lp = L + PAD
    NT = Lp // 128
    TPC = (NT - 1) // NC
    TB = 128 * (TPC + 1)
    depth = p["w_in"].shape[0]
    hp = np.zeros((Lp, D), np.float32)
    hp[PAD:PAD + N_META] = p["meta_tokens"]
    hp[PAD + N_META:] = x[0]
    hT = np.ascontiguousarray(hp.T)
    kc = host_consts()
    for l in range(depth):
        ncA = _prog(("A", NT), lambda: build_A(NT, 0))
        maps = [host_A_inputs(hT, p, l, c, kc) for c in range(NC)]
        res = run_bass_kernel_spmd(ncA, maps, core_ids=list(range(NC)))
        aoT = np.concatenate([res.results[c]["aoT"] for c in range(NC)], axis=0)
        doT = np.concatenate([res.results[c]["doT"] for c in range(NC)], axis=0)
        ncB = _prog(("B", TB), lambda: build_B(TB))
        maps = []
        for c in range(NC):
            a = 128 * TPC * c
            maps.append(host_B_inputs(hT[:, a:a + TB], aoT[:, a:a + TB], doT[:, a:a + TB], p, l))
        res = run_bass_kernel_spmd(ncB, maps, core_ids=list(range(NC)))
        hT_new = np.zeros_like(hT)
        hT_new[:, 0:128] = res.results[0]["outT"][:, 0:128]
        for c in range(NC):
            a = 128 * TPC * c
            hT_new[:, a + 128:a + TB] = res.results[c]["outT"][:, 128:]
        hT = hT_new
    out = np.ascontiguousarray(hT.T[PAD + N_META:]).reshape(1, SEQ, D).astype(np.float32)
    return out
```
